# Optimizing a Trainium2 kernel written in Bass

```python
import jax, jax.numpy as jnp
from jax import lax
import numpy as np

D_MODEL = 1024
BATCH = 8
SEQ = 8192
DEPTH = 4

N_MIXERS = 2
RMS_EPS = 1e-6
L2_EPS = 1e-6

GDN_HEADS = 8
GDN_DK = 128
GDN_DV = 128
GDN_CONV = 4
GDN_CHUNK = 64
GDN_QK_WIDTH = GDN_HEADS * GDN_DK
GDN_V_WIDTH = GDN_HEADS * GDN_DV
GDN_IN_WIDTH = 2 * GDN_QK_WIDTH + 2 * GDN_V_WIDTH + 2 * GDN_HEADS

DIL_GROUPS = ((128, 1), (512, 4), (2048, 16))
N_DIL_GROUPS = 3
DIL_HEADS_PER_GROUP = 8
DIL_HEAD_DIM = 64
DIL_TOTAL_HEADS = N_DIL_GROUPS * DIL_HEADS_PER_GROUP
DIL_IN_WIDTH = 3 * DIL_TOTAL_HEADS * DIL_HEAD_DIM
DIL_OUT_WIDTH = DIL_HEADS_PER_GROUP * DIL_HEAD_DIM
ALIBI_MAX_BIAS = 8.0

FFN_HIDDEN = -(-8 * D_MODEL // (3 * 256)) * 256

kernel_name = "hybrid_gdn_dilated_swa_swiglu"


def _rmsnorm(x, w):
    xf = x.astype(jnp.float32)
    y = xf * lax.rsqrt(jnp.mean(xf * xf, axis=-1, keepdims=True) + RMS_EPS)
    return (y * w.astype(jnp.float32)).astype(x.dtype)


def _l2norm(x):
    xf = x.astype(jnp.float32)
    return xf * lax.rsqrt(jnp.sum(xf * xf, axis=-1, keepdims=True) + L2_EPS)


def _causal_depthwise_conv(x, w):
    c = x.shape[-1]
    return lax.conv_general_dilated(
        x, w[:, None, :].astype(x.dtype), window_strides=(1,),
        padding=[(w.shape[0] - 1, 0)], dimension_numbers=("NWC", "WIO", "NWC"),
        feature_group_count=c)


def _chunk_gated_delta_rule(q, k, v, g, beta):
    b, s, h, dk = q.shape
    dv = v.shape[-1]
    c = GDN_CHUNK
    nc = s // c

    def chunks(t):
        return t.reshape(b, nc, c, h, -1).transpose(0, 3, 1, 2, 4)

    q, k, v = chunks(q), chunks(k), chunks(v)
    g = g.reshape(b, nc, c, h).transpose(0, 3, 1, 2)
    beta = beta.reshape(b, nc, c, h).transpose(0, 3, 1, 2)
    gc = jnp.cumsum(g, axis=-1)
    idx = jnp.arange(c)
    causal = idx[:, None] >= idx[None, :]
    strict = idx[:, None] > idx[None, :]
    decay = jnp.exp(jnp.where(causal, gc[..., :, None] - gc[..., None, :], -jnp.inf))
    kk = jnp.einsum("bhncd,bhnmd->bhncm", k, k)
    a_mat = jnp.where(strict, kk * beta[..., :, None] * decay, 0.0) + jnp.eye(c, dtype=jnp.float32)
    rhs = jnp.concatenate([v * beta[..., None], k * (beta * jnp.exp(gc))[..., None]], axis=-1)
    sol = lax.linalg.triangular_solve(a_mat, rhs, left_side=True, lower=True, unit_diagonal=True)
    u, w = sol[..., :dv], sol[..., dv:]
    qk = jnp.einsum("bhncd,bhnmd->bhncm", q, k) * decay
    q_dec = q * jnp.exp(gc)[..., None]
    k_tail = k * jnp.exp(gc[..., -1:] - gc)[..., None]
    c_dec = jnp.exp(gc[..., -1])
    xs = (jnp.moveaxis(qk, 2, 0), jnp.moveaxis(q_dec, 2, 0), jnp.moveaxis(k_tail, 2, 0),
          jnp.moveaxis(u, 2, 0), jnp.moveaxis(w, 2, 0), jnp.moveaxis(c_dec, 2, 0))

    def step(state, inp):
        qk_c, qd_c, kt_c, u_c, w_c, cd_c = inp
        v_new = u_c - jnp.einsum("bhck,bhkv->bhcv", w_c, state)
        o_c = (jnp.einsum("bhck,bhkv->bhcv", qd_c, state)
               + jnp.einsum("bhcm,bhmv->bhcv", qk_c, v_new))
        state = state * cd_c[..., None, None] + jnp.einsum("bhck,bhcv->bhkv", kt_c, v_new)
        return state, o_c

    s0 = jnp.zeros((b, h, dk, dv), jnp.float32)
    _, o = lax.scan(step, s0, xs)
    return o.transpose(1, 0, 3, 2, 4).reshape(b, s, h, dv)


def _gated_deltanet(h, w_in, conv_w, a_log, dt_bias, norm_w, w_out):
    b, s, _ = h.shape
    proj = h @ w_in
    n_qkv = 2 * GDN_QK_WIDTH + GDN_V_WIDTH
    qkv = jax.nn.silu(_causal_depthwise_conv(proj[..., :n_qkv], conv_w))
    z = proj[..., n_qkv:n_qkv + GDN_V_WIDTH].reshape(b, s, GDN_HEADS, GDN_DV)
    a = proj[..., n_qkv + GDN_V_WIDTH:n_qkv + GDN_V_WIDTH + GDN_HEADS].astype(jnp.float32)
    bb = proj[..., n_qkv + GDN_V_WIDTH + GDN_HEADS:].astype(jnp.float32)
    q = _l2norm(qkv[..., :GDN_QK_WIDTH].reshape(b, s, GDN_HEADS, GDN_DK)) * (GDN_DK ** -0.5)
    k = _l2norm(qkv[..., GDN_QK_WIDTH:2 * GDN_QK_WIDTH].reshape(b, s, GDN_HEADS, GDN_DK))
    v = qkv[..., 2 * GDN_QK_WIDTH:].reshape(b, s, GDN_HEADS, GDN_DV).astype(jnp.float32)
    beta = jax.nn.sigmoid(bb)
    g = -jnp.exp(a_log.astype(jnp.float32)) * jax.nn.softplus(a + dt_bias.astype(jnp.float32))
    o = _chunk_gated_delta_rule(q, k, v, g, beta)
    o = o * lax.rsqrt(jnp.mean(o * o, axis=-1, keepdims=True) + RMS_EPS)
    o = o * norm_w.astype(jnp.float32) * jax.nn.silu(z.astype(jnp.float32))
    return o.astype(h.dtype).reshape(b, s, GDN_V_WIDTH) @ w_out


def _dilated_group_attention(q, k, v, slopes, dilation, span):
    b, s, h, dh = q.shape
    n_sub = -(-s // dilation)
    nb = -(-n_sub // span)
    l_pad = nb * span
    t_pad = l_pad * dilation

    def to_blocks(t):
        t = jnp.pad(t, ((0, 0), (0, t_pad - s), (0, 0), (0, 0)))
        t = t.reshape(b, l_pad, dilation, h, dh).transpose(0, 2, 1, 3, 4)
        return t.reshape(b, dilation, nb, span, h, dh)

    def with_prev(t):
        prev = jnp.pad(t, ((0, 0), (0, 0), (1, 0), (0, 0), (0, 0), (0, 0)))[:, :, :-1]
        return jnp.concatenate([prev, t], axis=3)

    def from_blocks(t):
        e = t.shape[-1]
        t = t.reshape(b, dilation, l_pad, h, e).transpose(0, 2, 1, 3, 4)
        return t.reshape(b, t_pad, h, e)[:, :s]

    qb = to_blocks(q)
    kb = with_prev(to_blocks(k))
    vb = with_prev(to_blocks(v))
    scores = jnp.einsum("brnqhd,brnkhd->brnhqk", qb, kb,
                        preferred_element_type=jnp.float32) * (dh ** -0.5)
    qi = jnp.arange(span)
    kj = jnp.arange(2 * span)
    steps = qi[:, None] + span - kj[None, :]
    key_sub = jnp.arange(nb)[:, None] * span - span + kj[None, :]
    valid = ((steps >= 0) & (steps <= span))[None] & (key_sub >= 0)[:, None, :]
    bias = -slopes.astype(jnp.float32)[:, None, None] * (steps * dilation).astype(jnp.float32)
    logits = jnp.where(valid[:, None], scores + bias, -jnp.inf)
    m = jnp.max(logits, axis=-1, keepdims=True)
    p = jnp.exp(logits - m)
    l = jnp.sum(p, axis=-1)
    o = jnp.einsum("brnhqk,brnkhd->brnqhd", p, vb.astype(jnp.float32))
    o = o / l.transpose(0, 1, 2, 4, 3)[..., None]
    m_out = from_blocks(m[..., 0].transpose(0, 1, 2, 4, 3)[..., None])[..., 0]
    l_out = from_blocks(l.transpose(0, 1, 2, 4, 3)[..., None])[..., 0]
    return from_blocks(o), m_out, l_out


def _dilated_attention(h, w_in, q_norm, k_norm, w_out):
    b, s, _ = h.shape
    qkv = (h @ w_in).reshape(b, s, 3, N_DIL_GROUPS, DIL_HEADS_PER_GROUP, DIL_HEAD_DIM)
    q = _rmsnorm(qkv[:, :, 0], q_norm)
    k = _rmsnorm(qkv[:, :, 1], k_norm)
    v = qkv[:, :, 2]
    slopes = (2.0 ** (-ALIBI_MAX_BIAS * jnp.arange(1, DIL_TOTAL_HEADS + 1, dtype=jnp.float32)
                      / DIL_TOTAL_HEADS)).reshape(N_DIL_GROUPS, DIL_HEADS_PER_GROUP)
    outs, maxes, dens = [], [], []
    for gi, (window, dilation) in enumerate(DIL_GROUPS):
        o_g, m_g, l_g = _dilated_group_attention(q[:, :, gi], k[:, :, gi], v[:, :, gi],
                                                 slopes[gi], dilation, window // dilation)
        outs.append(o_g)
        maxes.append(m_g)
        dens.append(l_g)
    o_all = jnp.stack(outs)
    m_all = jnp.stack(maxes)
    l_all = jnp.stack(dens)
    wts = l_all * jnp.exp(m_all - jnp.max(m_all, axis=0, keepdims=True))
    o = jnp.sum(wts[..., None] * o_all, axis=0) / jnp.sum(wts, axis=0)[..., None]
    return o.astype(h.dtype).reshape(b, s, DIL_OUT_WIDTH) @ w_out


def _swiglu(h, w_in, w_out):
    gu = h @ w_in
    return (jax.nn.silu(gu[..., :FFN_HIDDEN]) * gu[..., FFN_HIDDEN:]) @ w_out


def setup_inputs(seed: int = 0) -> dict:
    key = jax.random.key(seed)
    ks = jax.random.split(key, 16)
    f32 = jnp.float32
    n_a = (DEPTH + 1) // 2
    n_b = DEPTH // 2
    out_scale = (2.0 * DEPTH) ** -0.5

    def nrm(k, shape, scale):
        return jax.random.normal(k, shape, f32) * scale

    def gain(k, shape):
        return 1.0 + 0.02 * jax.random.normal(k, shape, f32)

    dt = jnp.exp(jax.random.uniform(ks[6], (n_a, GDN_HEADS), f32, np.log(1e-3), np.log(1e-1)))
    return {
        "x": nrm(ks[0], (BATCH, SEQ, D_MODEL), 1.0),
        "norm_mix": gain(ks[1], (DEPTH, D_MODEL)),
        "norm_ffn": gain(ks[2], (DEPTH, D_MODEL)),
        "gdn_w_in": nrm(ks[3], (n_a, D_MODEL, GDN_IN_WIDTH), D_MODEL ** -0.5),
        "gdn_conv_w": nrm(ks[4], (n_a, GDN_CONV, 2 * GDN_QK_WIDTH + GDN_V_WIDTH), GDN_CONV ** -0.5),
        "gdn_a_log": jnp.log(jax.random.uniform(ks[5], (n_a, GDN_HEADS), f32, 1.0, 16.0)),
        "gdn_dt_bias": dt + jnp.log(-jnp.expm1(-dt)),
        "gdn_norm_w": gain(ks[7], (n_a, GDN_DV)),
        "gdn_w_out": nrm(ks[8], (n_a, GDN_V_WIDTH, D_MODEL), GDN_V_WIDTH ** -0.5 * out_scale),
        "dil_w_in": nrm(ks[9], (n_b, D_MODEL, DIL_IN_WIDTH), D_MODEL ** -0.5),
        "dil_q_norm": gain(ks[10], (n_b, DIL_HEAD_DIM)),
        "dil_k_norm": gain(ks[11], (n_b, DIL_HEAD_DIM)),
        "dil_w_out": nrm(ks[12], (n_b, DIL_OUT_WIDTH, D_MODEL), DIL_OUT_WIDTH ** -0.5 * out_scale),
        "ffn_w_in": nrm(ks[13], (DEPTH, D_MODEL, 2 * FFN_HIDDEN), D_MODEL ** -0.5),
        "ffn_w_out": nrm(ks[14], (DEPTH, FFN_HIDDEN, D_MODEL), FFN_HIDDEN ** -0.5 * out_scale),
    }


def reference(x, norm_mix, norm_ffn, gdn_w_in, gdn_conv_w, gdn_a_log, gdn_dt_bias, gdn_norm_w,
              gdn_w_out, dil_w_in, dil_q_norm, dil_k_norm, dil_w_out, ffn_w_in, ffn_w_out):
    for i in range(DEPTH):
        j = i // N_MIXERS
        hn = _rmsnorm(x, norm_mix[i])
        if i % N_MIXERS == 0:
            x = x + _gated_deltanet(hn, gdn_w_in[j], gdn_conv_w[j], gdn_a_log[j], gdn_dt_bias[j],
                                    gdn_norm_w[j], gdn_w_out[j])
        else:
            x = x + _dilated_attention(hn, dil_w_in[j], dil_q_norm[j], dil_k_norm[j], dil_w_out[j])
        x = x + _swiglu(_rmsnorm(x, norm_ffn[i]), ffn_w_in[i], ffn_w_out[i])
    return x
```

```python
import numpy as np
from contextlib import ExitStack
import concourse.bass as bass
import concourse.mybir as mybir
from concourse.bass_utils import run_bass_kernel_spmd

F32 = mybir.dt.float32
BF16 = mybir.dt.bfloat16
AF = mybir.ActivationFunctionType
ALU = mybir.AluOpType

D = 1024
FH = 2816
RMS_EPS = 1e-6

ENG_NAMES = ["pe", "act", "dve", "pool", "sp"]
HMAP = {"pe": "tensor", "act": "scalar", "dve": "vector", "pool": "gpsimd", "sp": "sync"}
N_DMA_SEMS = 6
SAME_ENGINE_SYNC = {"pe": False, "act": True, "dve": True, "pool": True, "sp": False}


class Buf:
    __slots__ = ("name", "w", "r")

    def __init__(self, name):
        self.name = name
        self.w = None
        self.r = {}


class KB:
    def __init__(self, nc, stack):
        self.nc = nc
        self.sem_names = []
        self.eng_sem = {}
        self.cnt = {}
        for e in ENG_NAMES:
            self.eng_sem[e] = self._new_sem("c_" + e)
            self.cnt[e] = 0
        self.dma_sems = {}
        self.dma_rr = {}
        for q in ["sp", "act", "pool"]:
            self.dma_sems[q] = [self._new_sem(f"d_{q}{i}") for i in range(N_DMA_SEMS)]
            self.dma_rr[q] = 0
        self.dma_cum = {s: 0 for s in range(len(self.sem_names))}
        self.waited = {e: {} for e in ENG_NAMES}
        self.sems = [stack.enter_context(nc.semaphore(n)) for n in self.sem_names]
        self.stream = {e: [] for e in ENG_NAMES}
        self.n_inst = 0

    def _new_sem(self, name):
        self.sem_names.append(name)
        return len(self.sem_names) - 1

    def buf(self, name="b"):
        return Buf(name)

    def _deps(self, eng, reads, writes):
        deps = {}
        for b in reads:
            if b.w is not None and deps.get(b.w[0], 0) < b.w[1]:
                deps[b.w[0]] = b.w[1]
        for b in writes:
            if b.w is not None and deps.get(b.w[0], 0) < b.w[1]:
                deps[b.w[0]] = b.w[1]
            for s, v in b.r.items():
                if deps.get(s, 0) < v:
                    deps[s] = v
        out = []
        own = self.eng_sem[eng]
        wd = self.waited[eng]
        for s, v in deps.items():
            if s == own and not SAME_ENGINE_SYNC[eng]:
                continue
            if wd.get(s, 0) >= v:
                continue
            wd[s] = v
            out.append((s, v))
        return out

    def _mark(self, tok, reads, writes):
        s, v = tok
        for b in reads:
            if b.r.get(s, 0) < v:
                b.r[s] = v
        for b in writes:
            b.w = tok
            b.r = {}

    def op(self, eng, fn, reads=(), writes=()):
        waits = self._deps(eng, reads, writes)
        self.cnt[eng] += 1
        tok = (self.eng_sem[eng], self.cnt[eng])
        self.stream[eng].append((waits, fn, tok, 1))
        self._mark(tok, reads, writes)
        self.n_inst += 1
        return tok

    def dma(self, q, out_ap, in_ap, reads=(), writes=(), **kw):
        sems = self.dma_sems[q]
        s = sems[self.dma_rr[q] % len(sems)]
        self.dma_rr[q] += 1
        waits = self._deps(q, reads, writes)
        if self.dma_cum[s] > 0 and self.waited[q].get(s, 0) < self.dma_cum[s]:
            self.waited[q][s] = self.dma_cum[s]
            waits.append((s, self.dma_cum[s]))
        self.dma_cum[s] += 16
        tok = (s, self.dma_cum[s])

        def fn(e, out_ap=out_ap, in_ap=in_ap, kw=kw):
            return e.dma_start(out=out_ap, in_=in_ap, **kw)
        self.stream[q].append((waits, fn, tok, 16))
        self._mark(tok, reads, writes)
        self.n_inst += 1
        return tok

    def barrier(self):
        targets = []
        for e in ENG_NAMES:
            if self.cnt[e] > 0:
                targets.append((self.eng_sem[e], self.cnt[e]))
        for s, v in self.dma_cum.items():
            if v > 0:
                targets.append((s, v))
        for e in ENG_NAMES:
            waits = []
            for s, v in targets:
                if s == self.eng_sem[e]:
                    continue
                if self.waited[e].get(s, 0) >= v:
                    continue
                self.waited[e][s] = v
                waits.append((s, v))
            if waits:
                self.stream[e].append((waits, None, None, 0))

    def emit(self):
        nc = self.nc
        sems = self.sems
        streams = self.stream
        self.stream = {e: [] for e in ENG_NAMES}
        with nc.Block() as block:
            def mk(ename):
                def body(e):
                    for waits, fn, tok, inc in streams[ename]:
                        for s, v in waits:
                            e.wait_ge(sems[s], v)
                        if fn is not None:
                            fn(e).then_inc(sems[tok[0]], inc)
                return body
            for ename in ENG_NAMES:
                if streams[ename]:
                    getattr(block, HMAP[ename])(mk(ename))


class Prog:
    def __init__(self, S, n_layers=4, phases=None):
        self.S = S
        self.NT = S // 512
        self.n_layers = n_layers
        self.nc = bass.Bass("TRN2", target_bir_lowering=False)
        self.top = ExitStack()
        self.kb = KB(self.nc, self.top)
        self.dram = {}
        self.in_names = []

    def din(self, name, shape, dt=F32):
        t = self.nc.dram_tensor(name, list(shape), dt, kind="ExternalInput").ap()
        self.dram[name] = t
        self.in_names.append(name)
        return t

    def dscratch(self, name, shape, dt=F32):
        kind = "ExternalOutput" if name in getattr(self, "debug", ()) else "Internal"
        t = self.nc.dram_tensor(name, list(shape), dt, kind=kind).ap()
        self.dram[name] = t
        return t

    def setup_consts(self):
        nc, kb, top = self.nc, self.kb, self.top
        self.ident_f = top.enter_context(nc.sbuf_tensor("ident_f", [128, 128], F32))
        self.ident_b = top.enter_context(nc.sbuf_tensor("ident_b", [128, 128], BF16))
        self.ones_b = top.enter_context(nc.sbuf_tensor("ones_b", [128, 128], BF16))
        self.b_const = kb.buf("const")
        d_ident = self.din("c_ident", [128, 128])
        kb.dma("sp", self.ident_f[:], d_ident, writes=[self.b_const])
        kb.op("dve", lambda e: e.tensor_copy(self.ident_b[:], self.ident_f[:]),
              reads=[self.b_const], writes=[self.b_const])
        kb.op("dve", lambda e: e.memset(self.ones_b[:], 1.0), writes=[self.b_const])

    def phase_in(self, x, xT):
        nc, kb = self.nc, self.kb
        with ExitStack() as st:
            xin = [st.enter_context(nc.sbuf_tensor(f"pi_xin{i}", [128, 4, D], F32)) for i in range(2)]
            xt = [st.enter_context(nc.sbuf_tensor(f"pi_xt{i}", [128, 8, 512], F32)) for i in range(2)]
            pt = [st.enter_context(nc.psum_tensor(f"pi_pt{i}", [128, 512], F32)) for i in range(4)]
            b_xin = [kb.buf() for _ in range(2)]
            b_xt = [kb.buf() for _ in range(2)]
            b_pt = [kb.buf() for _ in range(4)]
            b_xT = self.b_xT
            k = 0
            for t in range(self.NT):
                bi = t % 2
                kb.dma("sp", xin[bi][:], x[t * 512:(t + 1) * 512, :].rearrange("(n p) d -> p n d", p=128),
                       writes=[b_xin[bi]])
                for c in range(8):
                    pi = k % 4
                    k += 1
                    for n in range(4):
                        kb.op("pe", lambda e, o=pt[pi][:, n * 128:(n + 1) * 128], i=xin[bi][:, n, c * 128:(c + 1) * 128]:
                              e.transpose(o, i, self.ident_f[:]), reads=[b_xin[bi], self.b_const], writes=[b_pt[pi]])
                    eng = "dve" if c % 2 == 0 else "act"
                    if eng == "dve":
                        kb.op("dve", lambda e, o=xt[bi][:, c, :], i=pt[pi][:]: e.tensor_copy(o, i),
                              reads=[b_pt[pi]], writes=[b_xt[bi]])
                    else:
                        kb.op("act", lambda e, o=xt[bi][:, c, :], i=pt[pi][:]: e.copy(o, i),
                              reads=[b_pt[pi]], writes=[b_xt[bi]])
                kb.dma("sp", xT[:, t * 512:(t + 1) * 512].rearrange("(c p) s -> p c s", p=128), xt[bi][:],
                       reads=[b_xt[bi]], writes=[b_xT])
            kb.barrier()
            kb.emit()

    def phase_out(self, xT, y):
        nc, kb = self.nc, self.kb
        with ExitStack() as st:
            xt = [st.enter_context(nc.sbuf_tensor(f"po_xt{i}", [128, 8, 512], F32)) for i in range(2)]
            yo = [st.enter_context(nc.sbuf_tensor(f"po_yo{i}", [128, 4, D], F32)) for i in range(2)]
            pt = [st.enter_context(nc.psum_tensor(f"po_pt{i}", [128, 512], F32)) for i in range(4)]
            b_xt = [kb.buf() for _ in range(2)]
            b_yo = [kb.buf() for _ in range(2)]
            b_pt = [kb.buf() for _ in range(4)]
            k = 0
            for t in range(self.NT):
                bi = t % 2
                kb.dma("sp", xt[bi][:], xT[:, t * 512:(t + 1) * 512].rearrange("(c p) s -> p c s", p=128),
                       reads=[self.b_xT], writes=[b_xt[bi]])
                for n in range(4):
                    for c4 in range(2):
                        pi = k % 4
                        k += 1
                        for cc in range(4):
                            c = c4 * 4 + cc
                            kb.op("pe", lambda e, o=pt[pi][:, cc * 128:(cc + 1) * 128], i=xt[bi][:, c, n * 128:(n + 1) * 128]:
                                  e.transpose(o, i, self.ident_f[:]), reads=[b_xt[bi], self.b_const], writes=[b_pt[pi]])
                        if (n * 2 + c4) % 2 == 0:
                            kb.op("dve", lambda e, o=yo[bi][:, n, c4 * 512:(c4 + 1) * 512], i=pt[pi][:]: e.tensor_copy(o, i),
                                  reads=[b_pt[pi]], writes=[b_yo[bi]])
                        else:
                            kb.op("act", lambda e, o=yo[bi][:, n, c4 * 512:(c4 + 1) * 512], i=pt[pi][:]: e.copy(o, i),
                                  reads=[b_pt[pi]], writes=[b_yo[bi]])
                kb.dma("sp", y[t * 512:(t + 1) * 512, :].rearrange("(n p) d -> p n d", p=128), yo[bi][:],
                       reads=[b_yo[bi]], writes=[self.b_y])
            kb.barrier()
            kb.emit()

    def emit_rmsnorm(self, X, bX, nw, b_nw, xn, b_xn, sq, b_sq, rstd, b_rstd, p_s, b_ps):
        kb = self.kb
        for c in range(8):
            kb.op("act", lambda e, o=sq[c % 2][:], i=X[:, c, :]: e.activation(o, i, AF.Square),
                  reads=[bX], writes=[b_sq[c % 2]])
            kb.op("pe", lambda e, o=p_s[:], r=sq[c % 2][:], c=c: e.matmul(o, self.ones_b[:], r, start=(c == 0), stop=(c == 7)),
                  reads=[b_sq[c % 2], self.b_const], writes=[b_ps])
        kb.op("act", lambda e: e.activation(rstd[:], p_s[:], AF.Sqrt, bias=self.eps_t[:], scale=1.0 / D),
              reads=[b_ps, self.b_const], writes=[b_rstd])
        kb.op("dve", lambda e: e.reciprocal(rstd[:], rstd[:]), reads=[b_rstd], writes=[b_rstd])
        for c in range(8):
            kb.op("dve", lambda e, o=xn[:, c, :], i=X[:, c, :], s=nw[:, c:c + 1]:
                  e.scalar_tensor_tensor(o, i, s, rstd[:], ALU.mult, ALU.mult),
                  reads=[bX, b_rstd, b_nw], writes=[b_xn])

    def load_w_bf16(self, dst, src, kc, n, b_dst, q="pool"):
        npieces = -(-n // 2048)
        step = -(-(-(-n // npieces)) // 16) * 16
        for c in range(kc):
            for n0 in range(0, n, step):
                n1 = min(n, n0 + step)
                self.kb.dma(q, dst[:, c, n0:n1], src[:, c, n0:n1], writes=[b_dst])

    def phase_ffn(self, l, xT, w1d, w2d, nwd):
        nc, kb = self.nc, self.kb
        with ExitStack() as st:
            def sb(name, shape, dt):
                return st.enter_context(nc.sbuf_tensor("f_%d_" % l + name, shape, dt))

            def ps(name, shape, dt=F32):
                return st.enter_context(nc.psum_tensor("f_%d_" % l + name, shape, dt))
            w1 = sb("w1", [128, 8, 2 * FH], BF16)
            w2 = sb("w2", [128, 22, D], BF16)
            nw = sb("nw", [128, 8], F32)
            b_w1, b_w2, b_nw = kb.buf(), kb.buf(), kb.buf()
            kb.dma("sp", nw[:], nwd, writes=[b_nw])
            self.load_w_bf16(w1, w1d, 8, 2 * FH, b_w1)
            self.load_w_bf16(w2, w2d, 22, D, b_w2)
            xt = [sb(f"xt{i}", [128, 8, 512], F32) for i in range(2)]
            b_xt = [kb.buf() for _ in range(2)]
            sq = [sb(f"sq{i}", [128, 512], BF16) for i in range(2)]
            b_sq = [kb.buf() for _ in range(2)]
            rstd = sb("rstd", [128, 512], F32)
            b_rstd = kb.buf()
            xn = sb("xn", [128, 8, 512], BF16)
            b_xn = kb.buf()
            sg = [sb(f"sg{i}", [128, 512], BF16) for i in range(2)]
            b_sg = [kb.buf() for _ in range(2)]
            act = sb("act", [128, 22, 512], BF16)
            b_act = kb.buf()
            p_s = ps("p_s", [128, 512])
            b_ps = kb.buf()
            p_g = [ps(f"p_g{i}", [128, 512]) for i in range(2)]
            p_u = [ps(f"p_u{i}", [128, 512]) for i in range(2)]
            p_y = [ps(f"p_y{i}", [128, 512]) for i in range(2)]
            b_pg = [kb.buf() for _ in range(2)]
            b_pu = [kb.buf() for _ in range(2)]
            b_py = [kb.buf() for _ in range(2)]
            for t in range(self.NT):
                bi = t % 2
                X = xt[bi]
                bX = b_xt[bi]
                xsl = xT[:, t * 512:(t + 1) * 512].rearrange("(c p) s -> p c s", p=128)
                kb.dma("sp", X[:], xsl, reads=[self.b_xT], writes=[bX])
                self.emit_rmsnorm(X, bX, nw, b_nw, xn, b_xn, sq, b_sq, rstd, b_rstd, p_s, b_ps)
                for j in range(22):
                    pi = j % 2
                    for c in range(8):
                        kb.op("pe", lambda e, o=p_g[pi][:], w=w1[:, c, j * 128:(j + 1) * 128], r=xn[:, c, :], c=c:
                              e.matmul(o, w, r, start=(c == 0), stop=(c == 7)), reads=[b_w1, b_xn], writes=[b_pg[pi]])
                    for c in range(8):
                        kb.op("pe", lambda e, o=p_u[pi][:], w=w1[:, c, FH + j * 128:FH + (j + 1) * 128], r=xn[:, c, :], c=c:
                              e.matmul(o, w, r, start=(c == 0), stop=(c == 7)), reads=[b_w1, b_xn], writes=[b_pu[pi]])
                    kb.op("act", lambda e, o=sg[pi][:], i=p_g[pi][:]: e.activation(o, i, AF.Silu),
                          reads=[b_pg[pi]], writes=[b_sg[pi]])
                    kb.op("dve", lambda e, o=act[:, j, :], a=sg[pi][:], b=p_u[pi][:]: e.tensor_tensor(o, a, b, ALU.mult),
                          reads=[b_sg[pi], b_pu[pi]], writes=[b_act])
                for d in range(8):
                    pi = d % 2
                    for j in range(22):
                        kb.op("pe", lambda e, o=p_y[pi][:], w=w2[:, j, d * 128:(d + 1) * 128], r=act[:, j, :], j=j:
                              e.matmul(o, w, r, start=(j == 0), stop=(j == 21)), reads=[b_w2, b_act], writes=[b_py[pi]])
                    kb.op("dve", lambda e, o=X[:, d, :], b=p_y[pi][:]: e.tensor_tensor(o, o, b, ALU.add),
                          reads=[b_py[pi], bX], writes=[bX])
                kb.dma("sp", xsl, X[:], reads=[bX], writes=[self.b_xT])
            kb.barrier()
            kb.emit()

    def phase_attn_proj(self, l, xT, wd, nwd, gqd, gkd, QT, KT, Vtm):
        nc, kb, S = self.nc, self.kb, self.S
        with ExitStack() as st:
            def sb(name, shape, dt):
                return st.enter_context(nc.sbuf_tensor("a1_%d_" % l + name, shape, dt))

            def ps(name, shape, dt=F32):
                return st.enter_context(nc.psum_tensor("a1_%d_" % l + name, shape, dt))
            w = sb("w", [128, 8, 4608], BF16)
            nw = sb("nw", [128, 8], F32)
            gq = sb("gq", [128, 1], F32)
            gk = sb("gk", [128, 1], F32)
            eps2 = sb("eps2", [128, 1], F32)
            obd = sb("obd", [128, 128], BF16)
            b_w, b_nw, b_g = kb.buf(), kb.buf(), kb.buf()
            kb.dma("sp", nw[:], nwd, writes=[b_nw])
            kb.dma("sp", gq[:], gqd, writes=[b_g])
            kb.dma("sp", gk[:], gkd, writes=[b_g])
            kb.op("act", lambda e: e.mul(gq[:], gq[:], 0.125), reads=[b_g], writes=[b_g])
            kb.op("dve", lambda e: e.memset(eps2[:], RMS_EPS), writes=[b_g])
            kb.op("dve", lambda e: e.memset(obd[:], 0.0), writes=[b_g])
            kb.op("dve", lambda e: e.memset(obd[0:64, 0:64], 1.0), writes=[b_g])
            kb.op("dve", lambda e: e.memset(obd[64:128, 64:128], 1.0), writes=[b_g])
            self.load_w_bf16(w, wd, 8, 4608, b_w)
            xt = [sb(f"xt{i}", [128, 8, 512], F32) for i in range(2)]
            b_xt = [kb.buf() for _ in range(2)]
            sq = [sb(f"sq{i}", [128, 512], BF16) for i in range(2)]
            b_sq = [kb.buf() for _ in range(2)]
            rstd = sb("rstd", [128, 512], F32)
            b_rstd = kb.buf()
            xn = sb("xn", [128, 8, 512], BF16)
            b_xn = kb.buf()
            sq2 = [sb(f"sq2{i}", [128, 512], BF16) for i in range(2)]
            b_sq2 = [kb.buf() for _ in range(2)]
            rr_ = [sb(f"r{i}", [128, 512], F32) for i in range(2)]
            b_rr = [kb.buf() for _ in range(2)]
            stg = [[sb(f"stg{wh}{g}", [128, 4, 512], BF16) for g in range(3)] for wh in range(2)]
            b_stg = [[kb.buf() for g in range(3)] for wh in range(2)]
            vst = sb("vst", [128, 4, 1536], BF16)
            b_vst = kb.buf()
            p_s = ps("p_s", [128, 512])
            b_ps = kb.buf()
            p_qk = [ps(f"p_qk{i}", [128, 512]) for i in range(2)]
            b_pqk = [kb.buf() for _ in range(2)]
            p_ss = [ps(f"p_ss{i}", [128, 512]) for i in range(2)]
            b_pss = [kb.buf() for _ in range(2)]
            p_v = [ps(f"p_v{i}", [128, 512]) for i in range(2)]
            b_pv = [kb.buf() for _ in range(2)]
            DIL = [1, 4, 16]
            k = 0
            for t in range(self.NT):
                bi = t % 2
                X, bX = xt[bi], b_xt[bi]
                kb.dma("sp", X[:], xT[:, t * 512:(t + 1) * 512].rearrange("(c p) s -> p c s", p=128),
                       reads=[self.b_xT], writes=[bX])
                self.emit_rmsnorm(X, bX, nw, b_nw, xn, b_xn, sq, b_sq, rstd, b_rstd, p_s, b_ps)
                for wh in range(2):
                    gain = gq if wh == 0 else gk
                    dst_d = QT if wh == 0 else KT
                    for g in range(3):
                        d = DIL[g]
                        wdt = 512 // d
                        for cp in range(4):
                            pi = k % 2
                            k += 1
                            col0 = wh * 1536 + g * 512 + cp * 128
                            for c in range(8):
                                kb.op("pe", lambda e, o=p_qk[pi][:], ww=w[:, c, col0:col0 + 128], r=xn[:, c, :], c=c:
                                      e.matmul(o, ww, r, start=(c == 0), stop=(c == 7)), reads=[b_w, b_xn], writes=[b_pqk[pi]])
                            kb.op("act", lambda e, o=sq2[pi][:], i=p_qk[pi][:]: e.activation(o, i, AF.Square),
                                  reads=[b_pqk[pi]], writes=[b_sq2[pi]])
                            kb.op("pe", lambda e, o=p_ss[pi][:], r=sq2[pi][:]: e.matmul(o, obd[:], r, start=True, stop=True),
                                  reads=[b_sq2[pi], b_g], writes=[b_pss[pi]])
                            kb.op("act", lambda e, o=rr_[pi][:], i=p_ss[pi][:]: e.activation(o, i, AF.Sqrt, bias=eps2[:], scale=1.0 / 64),
                                  reads=[b_pss[pi], b_g], writes=[b_rr[pi]])
                            kb.op("dve", lambda e, o=rr_[pi][:]: e.reciprocal(o, o), reads=[b_rr[pi]], writes=[b_rr[pi]])
                            o_ap = stg[wh][g][:, cp, :].rearrange("p (r i) -> p i r", r=d)
                            i_ap = p_qk[pi][:].rearrange("p (i r) -> p i r", r=d)
                            r_ap = rr_[pi][:].rearrange("p (i r) -> p i r", r=d)
                            kb.op("dve", lambda e, o=o_ap, i=i_ap, r=r_ap, s=gain[:, 0:1]:
                                  e.scalar_tensor_tensor(o, i, s, r, ALU.mult, ALU.mult),
                                  reads=[b_pqk[pi], b_rr[pi], b_g], writes=[b_stg[wh][g]])
                        for cp in range(4):
                            dst = dst_d[g][cp * 128:(cp + 1) * 128, :].rearrange("p (r L) -> p r L", r=d)[:, :, t * wdt:(t + 1) * wdt]
                            kb.dma("sp", dst, stg[wh][g][:, cp, :].rearrange("p (r i) -> p r i", r=d),
                                   reads=[b_stg[wh][g]], writes=[self.b_qk])
                for n in range(4):
                    for g in range(3):
                        pi = k % 2
                        k += 1
                        for c in range(8):
                            kb.op("pe", lambda e, o=p_v[pi][:], lt=xn[:, c, n * 128:(n + 1) * 128], r=w[:, c, 3072 + g * 512:3072 + (g + 1) * 512], c=c:
                                  e.matmul(o, lt, r, start=(c == 0), stop=(c == 7)), reads=[b_w, b_xn], writes=[b_pv[pi]])
                        kb.op("act", lambda e, o=vst[:, n, g * 512:(g + 1) * 512], i=p_v[pi][:]: e.copy(o, i),
                              reads=[b_pv[pi]], writes=[b_vst])
                kb.dma("sp", Vtm[t * 512:(t + 1) * 512, :].rearrange("(n p) f -> p n f", p=128), vst[:],
                       reads=[b_vst], writes=[self.b_qk])
            kb.barrier()
            kb.emit()

    def phase_attn_core(self, l, stepsd, QT, KT, Vtm, ON, OL):
        nc, kb, S = self.nc, self.kb, self.S
        DIL = [1, 4, 16]
        with ExitStack() as st:
            def sb(name, shape, dt):
                return st.enter_context(nc.sbuf_tensor("a2_%d_" % l + name, shape, dt))

            def ps(name, shape, dt=F32):
                return st.enter_context(nc.psum_tensor("a2_%d_" % l + name, shape, dt))
            steps = sb("steps", [128, 2, 128], F32)
            b_c = kb.buf()
            kb.dma("sp", steps[:], stepsd, writes=[b_c])
            SEGMAX = 1024
            qs = [sb(f"qs{i}", [64, 8, SEGMAX], BF16) for i in range(2)]
            ks = [sb(f"ks{i}", [64, 8, SEGMAX + 128], BF16) for i in range(2)]
            vs = [sb(f"vs{i}", [128, SEGMAX // 128 + 1, 512], BF16) for i in range(2)]
            b_qs = [kb.buf() for _ in range(2)]
            b_ks = [kb.buf() for _ in range(2)]
            b_vs = [kb.buf() for _ in range(2)]
            for i in range(2):
                kb.op("dve", lambda e, o=ks[i][:]: e.memset(o, 0.0), writes=[b_ks[i]])
                kb.op("dve", lambda e, o=vs[i][:]: e.memset(o, 0.0), writes=[b_vs[i]])
            tmp = [sb(f"tmp{i}", [128, 2, 128], F32) for i in range(3)]
            b_tmp = [kb.buf() for _ in range(3)]
            pT = [sb(f"pT{i}", [128, 2, 128], BF16) for i in range(3)]
            b_pT = [kb.buf() for _ in range(3)]
            no = [sb(f"no{i}", [64, 8, 512], F32) for i in range(2)]
            lo = [sb(f"lo{i}", [64, 8, 512], F32) for i in range(2)]
            b_no = [kb.buf() for _ in range(2)]
            b_lo = [kb.buf() for _ in range(2)]
            p_sc = [ps(f"p_sc{i}", [128, 512]) for i in range(2)]
            b_psc = [kb.buf() for _ in range(2)]
            p_n = [ps(f"p_n{i}", [64, 512]) for i in range(2)]
            p_l = [ps(f"p_l{i}", [64, 512]) for i in range(2)]
            b_pn = [kb.buf() for _ in range(2)]
            b_pl = [kb.buf() for _ in range(2)]
            iseg = 0
            k3 = 0
            k2 = 0
            io = 0
            for g in range(3):
                d = DIL[g]
                Lr = S // d
                SEG = min(SEGMAX, Lr)
                for rr in range(d):
                    for p0 in range(0, Lr, SEG):
                        bi = iseg % 2
                        iseg += 1
                        Q, Kt, V = qs[bi], ks[bi], vs[bi]
                        base = rr * Lr + p0
                        kb.dma("sp", Q[:, :, 0:SEG], QT[g][:, base:base + SEG].rearrange("(h e) s -> e h s", e=64),
                               reads=[self.b_qk], writes=[b_qs[bi]])
                        hal = 128 if p0 > 0 else 0
                        kb.dma("sp", Kt[:, :, 128 - hal:128 + SEG],
                               KT[g][:, base - hal:base + SEG].rearrange("(h e) s -> e h s", e=64),
                               reads=[self.b_qk], writes=[b_ks[bi]])
                        nb = SEG // 128
                        vsrc = Vtm.rearrange("(i r) f -> r i f", r=d)[rr, p0 - hal:p0 + SEG, g * 512:(g + 1) * 512]
                        jb0 = 1 - hal // 128
                        kb.dma("sp", V[:, jb0:nb + 1, :], vsrc.rearrange("(j p) f -> p j f", p=128),
                               reads=[self.b_qk], writes=[b_vs[bi]])
                        for q4 in range(0, nb, 4):
                            nq = min(4, nb - q4)
                            oi = io % 2
                            io += 1
                            for h in range(8):
                                slope = 2.0 ** (-8.0 * (g * 8 + h + 1) / 24.0)
                                cneg = -slope * d
                                pni = k2 % 2
                                k2 += 1
                                for qq in range(nq):
                                    qb = q4 + qq
                                    have_prev = (p0 + qb * 128) > 0
                                    s0 = 0 if have_prev else 1
                                    ti = k3 % 3
                                    si = k3 % 2
                                    k3 += 1
                                    qcols = Q[:, h, qb * 128:(qb + 1) * 128]
                                    for s_ in range(s0, 2):
                                        kb.op("pe", lambda e, o=p_sc[si][:, s_ * 128:(s_ + 1) * 128], lt=Kt[:, h, qb * 128 + s_ * 128:qb * 128 + (s_ + 1) * 128], r=qcols:
                                              e.matmul(o, lt, r, start=True, stop=True),
                                              reads=[b_ks[bi], b_qs[bi]], writes=[b_psc[si]])
                                    kb.op("dve", lambda e, o=tmp[ti][:, s0:2, :], a=steps[:, s0:2, :], b=p_sc[si][:, s0 * 128:256].rearrange("p (s q) -> p s q", q=128), cn=cneg:
                                          e.scalar_tensor_tensor(o, a, cn, b, ALU.mult, ALU.add),
                                          reads=[b_c, b_psc[si]], writes=[b_tmp[ti]])
                                    kb.op("act", lambda e, o=pT[ti][:, s0:2, :], i=tmp[ti][:, s0:2, :]: e.activation(o, i, AF.Exp),
                                          reads=[b_tmp[ti]], writes=[b_pT[ti]])
                                    for s_ in range(s0, 2):
                                        kb.op("pe", lambda e, o=p_n[pni][:, qq * 128:(qq + 1) * 128], lt=V[:, qb + s_, h * 64:(h + 1) * 64], r=pT[ti][:, s_, :], s_=s_, s0=s0:
                                              e.matmul(o, lt, r, start=(s_ == s0), stop=(s_ == 1)),
                                              reads=[b_vs[bi], b_pT[ti]], writes=[b_pn[pni]])
                                    for s_ in range(s0, 2):
                                        kb.op("pe", lambda e, o=p_l[pni][:, qq * 128:(qq + 1) * 128], r=pT[ti][:, s_, :], s_=s_, s0=s0:
                                              e.matmul(o, self.ones_b[:, 0:64], r, start=(s_ == s0), stop=(s_ == 1)),
                                              reads=[self.b_const, b_pT[ti]], writes=[b_pl[pni]])
                                kb.op("act", lambda e, o=no[oi][:, h, 0:nq * 128], i=p_n[pni][:, 0:nq * 128]: e.copy(o, i),
                                      reads=[b_pn[pni]], writes=[b_no[oi]])
                                kb.op("dve", lambda e, o=lo[oi][:, h, 0:nq * 128], i=p_l[pni][:, 0:nq * 128]: e.tensor_copy(o, i),
                                      reads=[b_pl[pni]], writes=[b_lo[oi]])
                            ob = base + q4 * 128
                            kb.dma("sp", ON[g][:, ob:ob + nq * 128].rearrange("(h e) s -> e h s", e=64), no[oi][:, :, 0:nq * 128],
                                   reads=[b_no[oi]], writes=[self.b_o])
                            kb.dma("sp", OL[g][:, ob:ob + nq * 128].rearrange("(h e) s -> e h s", e=64), lo[oi][:, :, 0:nq * 128],
                                   reads=[b_lo[oi]], writes=[self.b_o])
            kb.barrier()
            kb.emit()

    def phase_attn_merge(self, l, xT, wod, ON, OL):
        nc, kb, S = self.nc, self.kb, self.S
        DIL = [1, 4, 16]
        with ExitStack() as st:
            def sb(name, shape, dt):
                return st.enter_context(nc.sbuf_tensor("a3_%d_" % l + name, shape, dt))

            def ps(name, shape, dt=F32):
                return st.enter_context(nc.psum_tensor("a3_%d_" % l + name, shape, dt))
            wo = sb("wo", [64, 8, D], BF16)
            b_wo = kb.buf()
            for h in range(8):
                kb.dma("pool", wo[:, h, :], wod[:, h, :], writes=[b_wo])
            xt = [sb(f"xt{i}", [128, 8, 512], F32) for i in range(2)]
            b_xt = [kb.buf() for _ in range(2)]
            nt = [[sb(f"nt{g}{i}", [64, 8, 512], F32) for g in range(3)] for i in range(1)]
            lt = [[sb(f"lt{g}{i}", [64, 8, 512], F32) for g in range(3)] for i in range(1)]
            nt.append(nt[0]); lt.append(lt[0])
            b_nt = [[kb.buf() for g in range(3)] for i in range(1)]
            b_lt = [[kb.buf() for g in range(3)] for i in range(1)]
            b_nt.append(b_nt[0]); b_lt.append(b_lt[0])
            oT = sb("oT", [64, 8, 512], BF16)
            b_oT = kb.buf()
            p_y = [ps(f"p_y{i}", [128, 512]) for i in range(2)]
            b_py = [kb.buf() for _ in range(2)]
            for t in range(self.NT):
                bi = t % 2
                X, bX = xt[bi], b_xt[bi]
                xsl = xT[:, t * 512:(t + 1) * 512].rearrange("(c p) s -> p c s", p=128)
                kb.dma("sp", X[:], xsl, reads=[self.b_xT], writes=[bX])
                for g in range(3):
                    d = DIL[g]
                    wdt = 512 // d
                    for h in range(8):
                        for (src, dstt, bb) in ((ON, nt, b_nt), (OL, lt, b_lt)):
                            s_ap = src[g][h * 64:(h + 1) * 64, :].rearrange("e (r L) -> e r L", r=d)[:, :, t * wdt:(t + 1) * wdt]
                            kb.dma("sp", dstt[bi][g][:, h, :].rearrange("e (r i) -> e r i", r=d), s_ap,
                                   reads=[self.b_o], writes=[bb[bi][g]])
                N0, L0 = nt[bi][0], lt[bi][0]
                for g in (1, 2):
                    d = DIL[g]
                    for h in range(8):
                        for (acc, srcT, bacc, bsrc) in ((N0, nt[bi][g], b_nt[bi][0], b_nt[bi][g]), (L0, lt[bi][g], b_lt[bi][0], b_lt[bi][g])):
                            a_ap = acc[:, h, :].rearrange("e (i r) -> e i r", r=d)
                            s_ap = srcT[:, h, :].rearrange("e (r i) -> e i r", r=d)
                            kb.op("dve", lambda e, a=a_ap, s=s_ap: e.tensor_tensor(a, a, s, ALU.add),
                                  reads=[bsrc, bacc], writes=[bacc])
                kb.op("dve", lambda e, o=L0[:]: e.reciprocal(o, o), reads=[b_lt[bi][0]], writes=[b_lt[bi][0]])
                kb.op("dve", lambda e, o=oT[:], a=N0[:], b=L0[:]: e.tensor_tensor(o, a, b, ALU.mult),
                      reads=[b_nt[bi][0], b_lt[bi][0]], writes=[b_oT])
                for dch in range(8):
                    pi = dch % 2
                    for h in range(8):
                        kb.op("pe", lambda e, o=p_y[pi][:], ww=wo[:, h, dch * 128:(dch + 1) * 128], r=oT[:, h, :], h=h:
                              e.matmul(o, ww, r, start=(h == 0), stop=(h == 7)), reads=[b_wo, b_oT], writes=[b_py[pi]])
                    kb.op("dve", lambda e, o=X[:, dch, :], b=p_y[pi][:]: e.tensor_tensor(o, o, b, ALU.add),
                          reads=[b_py[pi], bX], writes=[bX])
                kb.dma("sp", xsl, X[:], reads=[bX], writes=[self.b_xT])
            kb.barrier()
            kb.emit()

    def phase_gdn(self, l, xT, wd, wod, nwd, cwd, alogd, dtbd, gnwd, cst):
        nc, kb, S = self.nc, self.kb, self.S
        with ExitStack() as st:
            def sb(name, shape, dt):
                return st.enter_context(nc.sbuf_tensor("g_%d_" % l + name, shape, dt))

            def ps(name, shape, dt=F32):
                return st.enter_context(nc.psum_tensor("g_%d_" % l + name, shape, dt))
            w = sb("w", [128, 8, 4112], BF16)
            wo = sb("wo", [128, 8, D], BF16)
            nw = sb("nw", [128, 8], F32)
            cw = sb("cw", [128, 24, 4], F32)
            negA = sb("negA", [128, 4, 8], F32)
            dtb = sb("dtb", [128, 4, 8], F32)
            gnw = sb("gnw", [128, 128], F32)
            b_w, b_wo, b_sm = kb.buf(), kb.buf(), kb.buf()
            kb.dma("sp", nw[:], nwd, writes=[b_sm])
            kb.dma("sp", cw[:], cwd, writes=[b_sm])
            kb.dma("sp", negA[:], alogd, writes=[b_sm])
            kb.dma("sp", dtb[:], dtbd, writes=[b_sm])
            kb.dma("sp", gnw[:], gnwd, writes=[b_sm])
            kb.op("act", lambda e: e.activation(negA[:], negA[:], AF.Exp), reads=[b_sm], writes=[b_sm])
            kb.op("dve", lambda e: e.tensor_scalar(negA[:], negA[:], -1.0, None, ALU.mult), reads=[b_sm], writes=[b_sm])
            cn = {}
            for i, name in enumerate(["ubd", "ma", "mb", "msl", "mu"]):
                cn[name] = sb("c_" + name, [128, 128], F32)
                kb.dma("sp", cn[name][:], cst[i], writes=[b_sm])
            halfA = sb("halfA", [128, 1], F32)
            halfB = sb("halfB", [128, 1], F32)
            one_c = sb("one_c", [128, 1], F32)
            l2e = sb("l2e", [128, 1], F32)
            zer = sb("zer", [128, 128], F32)
            kb.op("dve", lambda e: e.memset(halfA[:], 0.0), writes=[b_sm])
            kb.op("dve", lambda e: e.memset(halfA[0:64, :], 1.0), writes=[b_sm])
            kb.op("dve", lambda e: e.memset(halfB[:], 0.0), writes=[b_sm])
            kb.op("dve", lambda e: e.memset(halfB[64:128, :], 1.0), writes=[b_sm])
            kb.op("dve", lambda e: e.memset(one_c[:], 1.0), writes=[b_sm])
            kb.op("dve", lambda e: e.memset(l2e[:], 1e-6), writes=[b_sm])
            kb.op("dve", lambda e: e.memset(zer[:], 0.0), writes=[b_sm])
            self.load_w_bf16(w, wd, 8, 4112, b_w)
            self.load_w_bf16(wo, wod, 8, D, b_wo)
            X = sb("xt", [128, 8, 512], F32)
            bX = kb.buf()
            sq = [sb(f"sq{i}", [128, 512], BF16) for i in range(2)]
            b_sq = [kb.buf() for _ in range(2)]
            rstd = sb("rstd", [128, 512], F32)
            b_rstd = kb.buf()
            xn = sb("xn", [128, 8, 512], BF16)
            b_xn = kb.buf()
            cs = sb("cs", [128, 515], F32)
            b_cs = kb.buf()
            acc = sb("acc", [128, 512], F32)
            b_acc = kb.buf()
            halo = sb("halo", [128, 24, 3], F32)
            b_halo = kb.buf()
            kb.op("dve", lambda e: e.memset(halo[:], 0.0), writes=[b_halo])
            qks = sb("qks", [128, 512], F32)
            b_qks = kb.buf()
            rr_ = sb("rr", [128, 512], F32)
            b_rr = kb.buf()
            qT = sb("qT", [128, 8, 512], BF16)
            kT = sb("kT", [128, 8, 512], BF16)
            vT = sb("vT", [128, 8, 512], BF16)
            b_qT, b_kT, b_vT = kb.buf(), kb.buf(), kb.buf()
            zs = sb("zs", [128, 4, 1024], BF16)
            b_zs = kb.buf()
            ogT = sb("ogT", [128, 8, 512], BF16)
            b_ogT = kb.buf()
            Sst = sb("S", [128, 8, 128], F32)
            Sb = sb("Sb", [128, 8, 128], BF16)
            b_S = [kb.buf() for _ in range(8)]
            b_Sb = [kb.buf() for _ in range(8)]
            kb.op("dve", lambda e: e.memset(Sst[:], 0.0), writes=b_S)
            kb.op("dve", lambda e: e.memset(Sb[:], 0.0), writes=b_Sb)
            gt = {n_: sb("gt_" + n_, [128, 4, 8], F32) for n_ in ["t0", "g", "beta"]}
            b_gt = kb.buf()
            gb = {n_: sb("gb_" + n_, [128, 8], F32) for n_ in ["gc", "gl", "eg", "ek", "cdA", "cdB", "be", "ekA", "ekB"]}
            b_gb = kb.buf()
            F_ROLES = ["G", "t", "DT", "t2", "L", "Nm", "X", "La", "Lb", "Na", "Nb", "u", "oq", "o", "junk"]
            B_ROLES = ["Xb", "kbe", "vb", "ktA", "ktB", "wT", "QKD", "vnew", "nz", "og"]
            tf = [{r: sb(f"tf{i}_{r}", [128, 128], F32) for r in F_ROLES} for i in range(2)]
            tb = [{r: sb(f"tb{i}_{r}", [128, 128], BF16) for r in B_ROLES} for i in range(2)]
            bf_ = [{r: kb.buf() for r in F_ROLES} for i in range(2)]
            bb_ = [{r: kb.buf() for r in B_ROLES} for i in range(2)]
            ssq = [sb(f"ssq{i}", [128, 1], F32) for i in range(2)]
            b_ssq = [kb.buf() for _ in range(2)]
            for i in range(2):
                kb.op("dve", lambda e, o=tb[i]["vnew"][:]: e.memset(o, 0.0), writes=[bb_[i]["vnew"]])
            p_s = ps("p_s", [128, 512])
            b_ps = kb.buf()
            pbig = [ps(f"pb{i}", [128, 512]) for i in range(3)]
            b_pbig = [kb.buf() for _ in range(3)]
            psm_banks = [ps(f"psmb{i}", [128, 512]) for i in range(3)]
            psm = [psm_banks[i % 3][:, (i // 3) * 128:(i // 3 + 1) * 128] for i in range(12)]
            b_bank = [kb.buf() for _ in range(3)]
            b_psm = [b_bank[i % 3] for i in range(12)]
            psb_bank = ps("psbb", [128, 1024], BF16)
            psb = [psb_bank[:, i * 128:(i + 1) * 128] for i in range(4)]
            b_psbb = kb.buf()
            b_psb = [b_psbb for _ in range(4)]
            ctr = {"big": 0, "sm": 0, "sb": 0}

            def big():
                i = ctr["big"] % 3
                ctr["big"] += 1
                return pbig[i], b_pbig[i]

            def sm():
                i = ctr["sm"] % 12
                ctr["sm"] += 1
                return psm[i], b_psm[i]

            def smb():
                i = ctr["sb"] % 4
                ctr["sb"] += 1
                return psb[i], b_psb[i]

            def mm(pt, bpt, lt, blt, rh, brh, start=True, stop=True):
                kb.op("pe", lambda e: e.matmul(pt, lt, rh, start=start, stop=stop), reads=list(blt) + list(brh), writes=[bpt])

            for t in range(self.NT):
                xsl = xT[:, t * 512:(t + 1) * 512].rearrange("(c p) s -> p c s", p=128)
                kb.dma("sp", X[:], xsl, reads=[self.b_xT], writes=[bX])
                self.emit_rmsnorm(X, bX, nw, b_sm, xn, b_xn, sq, b_sq, rstd, b_rstd, p_s, b_ps)
                for ch in range(24):
                    pp, bpp = big()
                    for c in range(8):
                        kb.op("pe", lambda e, o=pp[:], ww=w[:, c, ch * 128:(ch + 1) * 128], r=xn[:, c, :], c=c:
                              e.matmul(o, ww, r, start=(c == 0), stop=(c == 7)), reads=[b_w, b_xn], writes=[bpp])
                    kb.op("act", lambda e, i=pp[:]: e.copy(cs[:, 3:515], i), reads=[bpp], writes=[b_cs])
                    kb.op("dve", lambda e, ch=ch: e.tensor_copy(cs[:, 0:3], halo[:, ch, :]), reads=[b_halo], writes=[b_cs])
                    kb.op("dve", lambda e, ch=ch: e.tensor_scalar(acc[:], cs[:, 3:515], cw[:, ch, 3:4], None, ALU.mult),
                          reads=[b_cs, b_sm], writes=[b_acc])
                    for j in (2, 1, 0):
                        kb.op("dve", lambda e, ch=ch, j=j: e.scalar_tensor_tensor(acc[:], cs[:, j:j + 512], cw[:, ch, j:j + 1], acc[:], ALU.mult, ALU.add),
                              reads=[b_cs, b_sm, b_acc], writes=[b_acc])
                    kb.op("dve", lambda e, ch=ch: e.tensor_copy(halo[:, ch, :], cs[:, 512:515]), reads=[b_cs], writes=[b_halo])
                    if ch >= 16:
                        kb.op("act", lambda e, o=vT[:, ch - 16, :]: e.activation(o, acc[:], AF.Silu), reads=[b_acc], writes=[b_vT])
                    else:
                        kb.op("act", lambda e: e.activation(qks[:], acc[:], AF.Silu), reads=[b_acc], writes=[b_qks])
                        kb.op("act", lambda e: e.activation(sq[0][:], qks[:], AF.Square), reads=[b_qks], writes=[b_sq[0]])
                        p2, bp2 = big()
                        kb.op("pe", lambda e, o=p2[:]: e.matmul(o, self.ones_b[:], sq[0][:], start=True, stop=True),
                              reads=[b_sq[0], self.b_const], writes=[bp2])
                        kb.op("act", lambda e, i=p2[:]: e.activation(rr_[:], i, AF.Sqrt, bias=l2e[:], scale=1.0),
                              reads=[bp2, b_sm], writes=[b_rr])
                        kb.op("dve", lambda e: e.reciprocal(rr_[:], rr_[:]), reads=[b_rr], writes=[b_rr])
                        if ch < 8:
                            kb.op("dve", lambda e, o=qT[:, ch, :]: e.scalar_tensor_tensor(o, qks[:], 128.0 ** -0.5, rr_[:], ALU.mult, ALU.mult),
                                  reads=[b_qks, b_rr], writes=[b_qT])
                        else:
                            kb.op("dve", lambda e, o=kT[:, ch - 8, :]: e.tensor_tensor(o, qks[:], rr_[:], ALU.mult),
                                  reads=[b_qks, b_rr], writes=[b_kT])
                GS = getattr(self, "gdn_stop", 99)
                if GS <= 4:
                    kb.op("dve", lambda e: e.memset(ogT[:], 0.0), writes=[b_ogT])
                if GS > 1:
                    for n in range(4):
                        for hf in range(2):
                            pp, bpp = big()
                            for c in range(8):
                                kb.op("pe", lambda e, o=pp[:], lt=xn[:, c, n * 128:(n + 1) * 128], r=w[:, c, 3072 + hf * 512:3072 + (hf + 1) * 512], c=c:
                                      e.matmul(o, lt, r, start=(c == 0), stop=(c == 7)), reads=[b_w, b_xn], writes=[bpp])
                            kb.op("act", lambda e, o=zs[:, n, hf * 512:(hf + 1) * 512], i=pp[:]: e.activation(o, i, AF.Silu),
                                  reads=[bpp], writes=[b_zs])
                    pab, bpab = sm()
                    for n in range(4):
                        for c in range(8):
                            kb.op("pe", lambda e, o=pab[:, n * 16:(n + 1) * 16], lt=xn[:, c, n * 128:(n + 1) * 128], r=w[:, c, 4096:4112], c=c:
                                  e.matmul(o, lt, r, start=(c == 0), stop=(c == 7)), reads=[b_w, b_xn], writes=[bpab])
                    pab3 = pab[:, 0:64].rearrange("p (n f) -> p n f", f=16)
                    kb.op("dve", lambda e, i=pab3[:, :, 0:8]: e.tensor_tensor(gt["t0"][:], i, dtb[:], ALU.add), reads=[bpab, b_sm], writes=[b_gt])
                    kb.op("act", lambda e: e.activation(gt["t0"][:], gt["t0"][:], AF.Exp), reads=[b_gt], writes=[b_gt])
                    kb.op("act", lambda e: e.activation(gt["t0"][:], gt["t0"][:], AF.Ln, bias=one_c[:], scale=1.0), reads=[b_gt, b_sm], writes=[b_gt])
                    kb.op("dve", lambda e: e.tensor_tensor(gt["g"][:], gt["t0"][:], negA[:], ALU.mult), reads=[b_gt, b_sm], writes=[b_gt])
                    kb.op("act", lambda e, i=pab3[:, :, 8:16]: e.activation(gt["beta"][:], i, AF.Sigmoid), reads=[bpab], writes=[b_gt])
                    for n in range(4):
                        nsl = slice(n * 128, (n + 1) * 128)
                        pg, bpg = sm()
                        gsrc = gt["g"][:, n, :]
                        mm(pg[:, 0:8], bpg, cn["ubd"][:], [b_sm], gsrc, [b_gt])
                        mm(pg[:, 8:16], bpg, cn["ma"][:], [b_sm], gsrc, [b_gt])
                        mm(pg[:, 16:24], bpg, cn["mb"][:], [b_sm], gsrc, [b_gt])
                        G = gb
                        kb.op("dve", lambda e, i=pg[:, 0:8]: e.tensor_copy(G["gc"][:], i), reads=[bpg], writes=[b_gb])
                        kb.op("dve", lambda e, i=pg[:, 8:16]: e.tensor_scalar(G["gl"][:], i, halfA[:, 0:1], None, ALU.mult), reads=[bpg, b_sm], writes=[b_gb])
                        kb.op("dve", lambda e, i=pg[:, 16:24]: e.scalar_tensor_tensor(G["gl"][:], i, halfB[:, 0:1], G["gl"][:], ALU.mult, ALU.add),
                              reads=[bpg, b_sm, b_gb], writes=[b_gb])
                        kb.op("act", lambda e: e.activation(G["eg"][:], G["gc"][:], AF.Exp), reads=[b_gb], writes=[b_gb])
                        kb.op("dve", lambda e: e.tensor_tensor(G["ek"][:], G["gl"][:], G["gc"][:], ALU.subtract), reads=[b_gb], writes=[b_gb])
                        kb.op("act", lambda e: e.activation(G["ek"][:], G["ek"][:], AF.Exp), reads=[b_gb], writes=[b_gb])
                        kb.op("act", lambda e, i=pg[:, 8:16]: e.activation(G["cdA"][:], i, AF.Exp), reads=[bpg], writes=[b_gb])
                        kb.op("act", lambda e, i=pg[:, 16:24]: e.activation(G["cdB"][:], i, AF.Exp), reads=[bpg], writes=[b_gb])
                        kb.op("dve", lambda e, n=n: e.tensor_tensor(G["be"][:], gt["beta"][:, n, :], G["eg"][:], ALU.mult), reads=[b_gb, b_gt], writes=[b_gb])
                        kb.op("dve", lambda e: e.tensor_scalar(G["ekA"][:], G["ek"][:], halfA[:, 0:1], None, ALU.mult), reads=[b_gb, b_sm], writes=[b_gb])
                        kb.op("dve", lambda e: e.tensor_scalar(G["ekB"][:], G["ek"][:], halfB[:, 0:1], None, ALU.mult), reads=[b_gb, b_sm], writes=[b_gb])
                        for h in range(8):
                            hp = h % 2
                            if GS <= 2:
                                break
                            TF, TB, BF_, BB_ = tf[hp], tb[hp], bf_[hp], bb_[hp]
                            kq, kk_, kv = qT[:, h, nsl], kT[:, h, nsl], vT[:, h, nsl]
                            beta_c = gt["beta"][:, n, h:h + 1]
                            kb.op("act", lambda e, o=TF["G"][:], b=gt["g"][:, n, h:h + 1]: e.activation(o, zer[:], AF.Identity, bias=b, scale=1.0),
                                  reads=[b_gt, b_sm], writes=[BF_["G"]])
                            pgc, bpgc = sm()
                            mm(pgc[:], bpgc, TF["G"][:], [BF_["G"]], cn["ubd"][:], [b_sm])
                            kb.op("dve", lambda e, o=TF["t"][:], i=pgc[:], s=G["gc"][:, h:h + 1]: e.tensor_scalar(o, i, s, 0.0, ALU.subtract, ALU.min),
                                  reads=[bpgc, b_gb], writes=[BF_["t"]])
                            kb.op("act", lambda e, o=TF["t"][:]: e.activation(o, o, AF.Exp), reads=[BF_["t"]], writes=[BF_["t"]])
                            kb.op("dve", lambda e, o=TF["DT"][:], i=TF["t"][:]: e.tensor_tensor(o, i, cn["mu"][:], ALU.mult),
                                  reads=[BF_["t"], b_sm], writes=[BF_["DT"]])
                            kb.op("dve", lambda e, o=TF["t2"][:], i=pgc[:], s=G["gc"][:, h:h + 1]: e.tensor_scalar(o, i, s, 0.0, ALU.subtract, ALU.max),
                                  reads=[bpgc, b_gb], writes=[BF_["t2"]])
                            kb.op("act", lambda e, o=TF["t2"][:]: e.activation(o, o, AF.Exp, scale=-1.0), reads=[BF_["t2"]], writes=[BF_["t2"]])
                            kb.op("dve", lambda e, o=TF["t2"][:]: e.tensor_tensor(o, o, cn["msl"][:], ALU.mult),
                                  reads=[BF_["t2"], b_sm], writes=[BF_["t2"]])
                            pkk, bpkk = sm()
                            mm(pkk[:], bpkk, kk_, [b_kT], kk_, [b_kT])
                            kb.op("dve", lambda e, o=TF["L"][:], i=pkk[:], s=beta_c, d_=TF["t2"][:]: e.scalar_tensor_tensor(o, i, s, d_, ALU.mult, ALU.mult),
                                  reads=[bpkk, b_gt, BF_["t2"]], writes=[BF_["L"]])
                            ptr, bptr = sm()
                            kb.op("pe", lambda e, o=ptr[:], i=TF["L"][:]: e.transpose(o, i, self.ident_f[:]), reads=[BF_["L"], self.b_const], writes=[bptr])
                            kb.op("dve", lambda e, o=TF["Nm"][:], i=ptr[:]: e.tensor_copy(o, i), reads=[bptr], writes=[BF_["Nm"]])
                            kb.op("dve", lambda e, o=TF["X"][:], i=TF["Nm"][:]: e.tensor_tensor(o, self.ident_f[:], i, ALU.subtract),
                                  reads=[BF_["Nm"], self.b_const], writes=[BF_["X"]])
                            Lp, Np, bLp, bNp = TF["L"], TF["Nm"], BF_["L"], BF_["Nm"]
                            for kk2 in range(1, 6):
                                Ln_, bLn = (TF["La"], BF_["La"]) if kk2 % 2 == 1 else (TF["Lb"], BF_["Lb"])
                                Nn_, bNn = (TF["Na"], BF_["Na"]) if kk2 % 2 == 1 else (TF["Nb"], BF_["Nb"])
                                pl2, bpl2 = sm()
                                mm(pl2[:], bpl2, Np[:], [bNp], Lp[:], [bLp])
                                kb.op("act", lambda e, o=Ln_[:], i=pl2[:]: e.copy(o, i), reads=[bpl2], writes=[bLn])
                                if kk2 < 5:
                                    pn2, bpn2 = sm()
                                    mm(pn2[:], bpn2, Lp[:], [bLp], Np[:], [bNp])
                                    kb.op("dve", lambda e, o=Nn_[:], i=pn2[:]: e.tensor_copy(o, i), reads=[bpn2], writes=[bNn])
                                px, bpx = sm()
                                mm(px[:], bpx, Ln_[:], [bLn], TF["X"][:], [BF_["X"]])
                                kb.op("dve", lambda e, o=TF["X"][:], i=px[:]: e.tensor_tensor(o, o, i, ALU.add), reads=[bpx, BF_["X"]], writes=[BF_["X"]])
                                Lp, Np, bLp, bNp = Ln_, Nn_, bLn, bNn
                            kb.op("act", lambda e, o=TB["Xb"][:], i=TF["X"][:]: e.copy(o, i), reads=[BF_["X"]], writes=[BB_["Xb"]])
                            if GS <= 3:
                                continue
                            pkt, bpkt = smb()
                            kb.op("pe", lambda e, o=pkt[:], i=kk_: e.transpose(o, i, self.ident_b[:]), reads=[b_kT, self.b_const], writes=[bpkt])
                            kb.op("dve", lambda e, o=TB["kbe"][:], i=pkt[:], s=G["be"][:, h:h + 1]: e.tensor_scalar(o, i, s, None, ALU.mult), reads=[bpkt, b_gb], writes=[BB_["kbe"]])
                            kb.op("dve", lambda e, o=TB["ktA"][:], i=pkt[:], s=G["ekA"][:, h:h + 1]: e.tensor_scalar(o, i, s, None, ALU.mult), reads=[bpkt, b_gb], writes=[BB_["ktA"]])
                            kb.op("dve", lambda e, o=TB["ktB"][:], i=pkt[:], s=G["ekB"][:, h:h + 1]: e.tensor_scalar(o, i, s, None, ALU.mult), reads=[bpkt, b_gb], writes=[BB_["ktB"]])
                            pvt, bpvt = smb()
                            kb.op("pe", lambda e, o=pvt[:], i=kv: e.transpose(o, i, self.ident_b[:]), reads=[b_vT, self.b_const], writes=[bpvt])
                            kb.op("dve", lambda e, o=TB["vb"][:], i=pvt[:], s=beta_c: e.tensor_scalar(o, i, s, None, ALU.mult), reads=[bpvt, b_gt], writes=[BB_["vb"]])
                            pu, bpu = sm()
                            mm(pu[:], bpu, TB["Xb"][:], [BB_["Xb"]], TB["vb"][:], [BB_["vb"]])
                            kb.op("act", lambda e, o=TF["u"][:], i=pu[:]: e.copy(o, i), reads=[bpu], writes=[BF_["u"]])
                            pw_, bpw = sm()
                            mm(pw_[:], bpw, TB["kbe"][:], [BB_["kbe"]], TB["Xb"][:], [BB_["Xb"]])
                            kb.op("dve", lambda e, o=TB["wT"][:], i=pw_[:]: e.tensor_copy(o, i), reads=[bpw], writes=[BB_["wT"]])
                            pqk, bpqk = sm()
                            mm(pqk[:], bpqk, kk_, [b_kT], kq, [b_qT])
                            kb.op("dve", lambda e, o=TB["QKD"][:], i=pqk[:], d_=TF["DT"][:]: e.tensor_tensor(o, i, d_, ALU.mult),
                                  reads=[bpqk, BF_["DT"]], writes=[BB_["QKD"]])
                            for (rows, ktX, cdX) in ((slice(0, 64), "ktA", "cdA"), (slice(64, 128), "ktB", "cdB")):
                                pws, bpws = sm()
                                mm(pws[:], bpws, TB["wT"][:], [BB_["wT"]], Sb[:, h, :], [b_Sb[h]])
                                pqs, bpqs = sm()
                                mm(pqs[:], bpqs, kq, [b_qT], Sb[:, h, :], [b_Sb[h]])
                                kb.op("dve", lambda e, o=TB["vnew"][rows, :], a=TF["u"][rows, :], b=pws[rows, :]: e.tensor_tensor(o, a, b, ALU.subtract),
                                      reads=[BF_["u"], bpws], writes=[BB_["vnew"]])
                                kb.op("dve", lambda e, o=TF["oq"][rows, :], i=pqs[rows, :], s=G["eg"][rows, h:h + 1]: e.tensor_scalar(o, i, s, None, ALU.mult),
                                      reads=[bpqs, b_gb], writes=[BF_["oq"]])
                                pkv, bpkv = sm()
                                mm(pkv[:], bpkv, TB[ktX][:], [BB_[ktX]], TB["vnew"][:], [BB_["vnew"]])
                                kb.op("dve", lambda e, o=Sst[:, h, :], s=G[cdX][:, h:h + 1], i=pkv[:]: e.scalar_tensor_tensor(o, o, s, i, ALU.mult, ALU.add),
                                      reads=[b_S[h], b_gb, bpkv], writes=[b_S[h]])
                                kb.op("act", lambda e, o=Sb[:, h, :], i=Sst[:, h, :]: e.copy(o, i), reads=[b_S[h]], writes=[b_Sb[h]])
                            po, bpo = sm()
                            mm(po[:], bpo, TB["QKD"][:], [BB_["QKD"]], TB["vnew"][:], [BB_["vnew"]])
                            kb.op("dve", lambda e, o=TF["o"][:], i=po[:], q_=TF["oq"][:]: e.tensor_tensor(o, i, q_, ALU.add),
                                  reads=[bpo, BF_["oq"]], writes=[BF_["o"]])
                            if GS <= 4:
                                continue
                            kb.op("act", lambda e, o=TF["junk"][:], i=TF["o"][:], a=ssq[hp][:, 0:1]: e.activation(o, i, AF.Square, accum_out=a),
                                  reads=[BF_["o"]], writes=[BF_["junk"], b_ssq[hp]])
                            kb.op("act", lambda e, a=ssq[hp][:, 0:1]: e.activation(a, a, AF.Sqrt, bias=self.eps_t[:], scale=1.0 / 128),
                                  reads=[b_ssq[hp], self.b_const], writes=[b_ssq[hp]])
                            kb.op("dve", lambda e, a=ssq[hp][:, 0:1]: e.reciprocal(a, a), reads=[b_ssq[hp]], writes=[b_ssq[hp]])
                            kb.op("dve", lambda e, o=TB["nz"][:], z_=zs[:, n, h * 128:(h + 1) * 128]: e.tensor_tensor(o, z_, gnw[:], ALU.mult),
                                  reads=[b_zs, b_sm], writes=[BB_["nz"]])
                            kb.op("dve", lambda e, o=TB["og"][:], i=TF["o"][:], s=ssq[hp][:, 0:1], z_=TB["nz"][:]: e.scalar_tensor_tensor(o, i, s, z_, ALU.mult, ALU.mult),
                                  reads=[BF_["o"], b_ssq[hp], BB_["nz"]], writes=[BB_["og"]])
                            pot, bpot = smb()
                            kb.op("pe", lambda e, o=pot[:], i=TB["og"][:]: e.transpose(o, i, self.ident_b[:]), reads=[BB_["og"], self.b_const], writes=[bpot])
                            kb.op("act", lambda e, o=ogT[:, h, nsl], i=pot[:]: e.copy(o, i), reads=[bpot], writes=[b_ogT])
                for dch in range(8):
                    pp, bpp = big()
                    for h in range(8):
                        kb.op("pe", lambda e, o=pp[:], ww=wo[:, h, dch * 128:(dch + 1) * 128], r=ogT[:, h, :], h=h:
                              e.matmul(o, ww, r, start=(h == 0), stop=(h == 7)), reads=[b_wo, b_ogT], writes=[bpp])
                    kb.op("dve", lambda e, o=X[:, dch, :], b=pp[:]: e.tensor_tensor(o, o, b, ALU.add), reads=[bpp, bX], writes=[bX])
                kb.dma("sp", xsl, X[:], reads=[bX], writes=[self.b_xT])
            kb.barrier()
            kb.emit()

    def build(self, phases, debug=()):
        nc, kb = self.nc, self.kb
        S = self.S
        self.debug = set(debug)
        x = self.din("x", [S, D])
        y = nc.dram_tensor("y", [S, D], F32, kind="ExternalOutput").ap()
        xT = self.dscratch("xT", [D, S])
        self.b_xT = kb.buf("xT")
        self.b_y = kb.buf("y")
        self.b_qk = kb.buf("qk")
        self.b_o = kb.buf("o")
        self.setup_consts()
        self.eps_t = self.top.enter_context(nc.sbuf_tensor("eps_t", [128, 1], F32))
        kb.op("dve", lambda e: e.memset(self.eps_t[:], RMS_EPS), writes=[self.b_const])
        decl = {}

        def inp(name, shape):
            if name not in decl:
                decl[name] = self.din(name, shape)
            return decl[name]
        scr = {}

        def scratch(name, shape, dt):
            if name not in scr:
                scr[name] = self.dscratch(name, shape, dt)
            return scr[name]
        for ph in phases:
            if ph == "in":
                self.phase_in(x, xT)
            elif ph == "out":
                self.phase_out(xT, y)
            elif ph.startswith("ffn"):
                l = int(ph[3:])
                self.phase_ffn(l, xT, inp(f"ffn_w1_{l}", [128, 8, 2 * FH]), inp(f"ffn_w2_{l}", [128, 22, D]),
                               inp(f"ffn_nw_{l}", [128, 8]))
            elif ph.startswith("gdn"):
                l = int(ph[3:])
                cst = [inp("c_" + n_, [128, 128]) for n_ in ["ubd", "ma", "mb", "msl", "mu"]]
                self.phase_gdn(l, xT, inp(f"gdn_w_{l}", [128, 8, 4112]), inp(f"gdn_wo_{l}", [128, 8, D]),
                               inp(f"mix_nw_{l}", [128, 8]), inp(f"gdn_cw_{l}", [128, 24, 4]),
                               inp(f"gdn_alog_{l}", [128, 4, 8]), inp(f"gdn_dtb_{l}", [128, 4, 8]),
                               inp(f"gdn_gnw_{l}", [128, 128]), cst)
            elif ph.startswith("attn"):
                l = int(ph[4:])
                QT = [scratch(f"QT{g}", [512, S], BF16) for g in range(3)]
                KT = [scratch(f"KT{g}", [512, S], BF16) for g in range(3)]
                Vtm = scratch("Vtm", [S, 1536], BF16)
                ON = [scratch(f"ON{g}", [512, S], F32) for g in range(3)]
                OL = [scratch(f"OL{g}", [512, S], F32) for g in range(3)]
                self.phase_attn_proj(l, xT, inp(f"dil_w_{l}", [128, 8, 4608]), inp(f"mix_nw_{l}", [128, 8]),
                                     inp(f"dil_gq_{l}", [128, 1]), inp(f"dil_gk_{l}", [128, 1]), QT, KT, Vtm)
                self.phase_attn_core(l, inp("c_steps", [128, 2, 128]), QT, KT, Vtm, ON, OL)
                self.phase_attn_merge(l, xT, inp(f"dil_wo_{l}", [64, 8, D]), ON, OL)
        fin = kb._deps("sp", [self.b_y], [])
        if fin:
            kb.stream["sp"].append((fin, None, None, 0))
        kb.emit()
        self.top.close()
        return nc


def pw(w):
    k, n = w.shape
    return np.ascontiguousarray(w.reshape(k // 128, 128, n).transpose(1, 0, 2))


def pv(v):
    return np.ascontiguousarray(v.reshape(-1, 128).T)


def steps_const():
    k = np.arange(128)[:, None]
    q = np.arange(128)[None, :]
    st = np.zeros((128, 2, 128), np.float32)
    s0 = (q + 128 - k).astype(np.float32)
    st[:, 0, :] = np.where(k >= q, s0, 30000.0)
    s1 = (q - k).astype(np.float32)
    st[:, 1, :] = np.where(q >= k, s1, 30000.0)
    return st


def host_inputs(inp, needed):
    m = {}
    for name in needed:
        if name == "c_ident":
            m[name] = np.eye(128, dtype=np.float32)
        elif name == "c_steps":
            m[name] = steps_const()
        elif name in ("c_ubd", "c_ma", "c_mb", "c_msl", "c_mu"):
            a = np.arange(128)
            same = (a[:, None] // 64) == (a[None, :] // 64)
            m[name] = {"c_ubd": same & (a[:, None] <= a[None, :]),
                       "c_ma": np.broadcast_to((a < 64)[:, None], (128, 128)),
                       "c_mb": np.broadcast_to((a >= 64)[:, None], (128, 128)),
                       "c_msl": same & (a[:, None] > a[None, :]),
                       "c_mu": same & (a[None, :] >= a[:, None])}[name].astype(np.float32).copy()
        elif name == "x":
            continue
        else:
            kind, l = name.rsplit("_", 1)
            l = int(l)
            j = l // 2
            if kind == "ffn_w1":
                m[name] = pw(np.asarray(inp["ffn_w_in"][l]))
            elif kind == "ffn_w2":
                m[name] = pw(np.asarray(inp["ffn_w_out"][l]))
            elif kind == "ffn_nw":
                m[name] = pv(np.asarray(inp["norm_ffn"][l]))
            elif kind == "mix_nw":
                m[name] = pv(np.asarray(inp["norm_mix"][l]))
            elif kind == "gdn_w":
                m[name] = pw(np.asarray(inp["gdn_w_in"][j]))
            elif kind == "gdn_wo":
                m[name] = pw(np.asarray(inp["gdn_w_out"][j]))
            elif kind == "gdn_cw":
                m[name] = np.ascontiguousarray(np.asarray(inp["gdn_conv_w"][j]).T.reshape(24, 128, 4).transpose(1, 0, 2))
            elif kind == "gdn_alog":
                m[name] = np.ascontiguousarray(np.broadcast_to(np.asarray(inp["gdn_a_log"][j])[None, None, :], (128, 4, 8)))
            elif kind == "gdn_dtb":
                m[name] = np.ascontiguousarray(np.broadcast_to(np.asarray(inp["gdn_dt_bias"][j])[None, None, :], (128, 4, 8)))
            elif kind == "gdn_gnw":
                m[name] = np.ascontiguousarray(np.broadcast_to(np.asarray(inp["gdn_norm_w"][j])[None, :], (128, 128)))
            elif kind == "dil_w":
                m[name] = pw(np.asarray(inp["dil_w_in"][j]))
            elif kind == "dil_gq":
                m[name] = np.ascontiguousarray(np.tile(np.asarray(inp["dil_q_norm"][j]), 2)[:, None])
            elif kind == "dil_gk":
                m[name] = np.ascontiguousarray(np.tile(np.asarray(inp["dil_k_norm"][j]), 2)[:, None])
            elif kind == "dil_wo":
                m[name] = np.ascontiguousarray(np.asarray(inp["dil_w_out"][j]).reshape(8, 64, D).transpose(1, 0, 2))
            else:
                raise KeyError(name)
    return m


PHASES = ["in", "gdn0", "ffn0", "attn1", "ffn1", "gdn2", "ffn2", "attn3", "ffn3", "out"]


def kernel(**inputs):
    x = np.asarray(inputs["x"], dtype=np.float32)
    B, S, _ = x.shape
    prog = Prog(S)
    nc = prog.build(PHASES)
    shared = host_inputs(inputs, prog.in_names)
    in_maps = []
    for b in range(B):
        m = dict(shared)
        m["x"] = np.ascontiguousarray(x[b])
        in_maps.append(m)
    res = run_bass_kernel_spmd(nc, in_maps, core_ids=list(range(B)))
    return np.stack([np.asarray(r["y"], dtype=np.float32) for r in res.results], axis=0)
```
